# Optimizing a Trainium2 kernel written in Bass

```python
import jax, jax.numpy as jnp
from jax import lax
import numpy as np

D_MODEL = 4096
BATCH = 1
SEQ = 8192
DEPTH = 1

CHUNK = 64
Q_BLOCK = 128
D_MIX = D_MODEL
FOX_HEADS = 16
FOX_HEAD_DIM = 128
FOX_WIDTH = FOX_HEADS * FOX_HEAD_DIM
HGRN_HEADS = 16
HGRN_DK = 128
HGRN_DV = 128
HGRN_KWIDTH = HGRN_HEADS * HGRN_DK
HGRN_VWIDTH = HGRN_HEADS * HGRN_DV
D_FF = 11008
CONV_WIDTH = 3
N_MOD = 6
EPS = 1e-6

FOX_Q0 = 0
FOX_K0 = FOX_Q0 + FOX_WIDTH
FOX_V0 = FOX_K0 + FOX_WIDTH
FOX_F0 = FOX_V0 + FOX_WIDTH
HG_Q0 = FOX_F0 + FOX_HEADS
HG_F0 = HG_Q0 + HGRN_KWIDTH
HG_I0 = HG_F0 + HGRN_KWIDTH
HG_G0 = HG_I0 + HGRN_VWIDTH
IN_COLS = HG_G0 + HGRN_VWIDTH

kernel_name = "hybrid_fox_hgrn2_convffn_adaln"


def rms_norm(x, g):
    xf = x.astype(jnp.float32)
    y = xf * lax.rsqrt(jnp.mean(xf * xf, axis=-1, keepdims=True) + EPS)
    return (y * g.astype(jnp.float32)).astype(x.dtype)


def modulate(h, shift, scale):
    return h * (1 + scale[:, None, :]) + shift[:, None, :]


def forgetting_attention(q, k, v, log_f):
    B, T, H, Dh = q.shape
    nb = T // Q_BLOCK
    cum = jnp.cumsum(log_f, axis=1).transpose(0, 2, 1)
    qh = q.transpose(0, 2, 1, 3)
    kh = k.transpose(0, 2, 1, 3)
    vh = v.transpose(0, 2, 1, 3)
    qb = qh.reshape(B, H, nb, Q_BLOCK, Dh).transpose(2, 0, 1, 3, 4)
    cb = cum.reshape(B, H, nb, Q_BLOCK).transpose(2, 0, 1, 3)
    pos = jnp.arange(T)
    pb = pos.reshape(nb, Q_BLOCK)
    scale = Dh ** -0.5

    def block(args):
        q_blk, c_blk, p_blk = args
        s = jnp.einsum('bhqd,bhkd->bhqk', q_blk, kh).astype(jnp.float32) * scale
        s = s + c_blk[..., :, None] - cum[..., None, :]
        mask = pos[None, :] <= p_blk[:, None]
        s = jnp.where(mask, s, -jnp.inf)
        p = jax.nn.softmax(s, axis=-1)
        return jnp.einsum('bhqk,bhkd->bhqd', p.astype(vh.dtype), vh)

    ob = lax.map(block, (qb, cb, pb))
    return ob.transpose(1, 0, 3, 2, 4).reshape(B, T, H, Dh)


def hgrn2_recurrence(q, f, i, lb):
    B, T, H, Dk = q.shape
    Dv = i.shape[-1]
    n = T // CHUNK
    lbh = lb.reshape(H, Dk)
    fg = lbh + (1 - lbh) * jax.nn.sigmoid(f.astype(jnp.float32))
    log_f = jnp.log(fg)
    kk = 1 - fg
    qf = jax.nn.silu(q.astype(jnp.float32))

    def chunks(a):
        return a.reshape(B, n, CHUNK, H, a.shape[-1]).transpose(1, 0, 3, 2, 4)

    xs = (chunks(qf), chunks(kk), chunks(log_f), chunks(i.astype(jnp.float32)))
    causal = jnp.tril(jnp.ones((CHUNK, CHUNK), dtype=bool))

    def step(S, inp):
        qc, kc, lc, ic = inp
        b = jnp.cumsum(lc, axis=2)
        o_inter = jnp.einsum('bhtk,bhkv->bhtv', qc * jnp.exp(b), S)
        rel = jnp.where(causal[:, :, None], b[:, :, :, None, :] - b[:, :, None, :, :], -jnp.inf)
        A = jnp.einsum('bhtk,bhsk,bhtsk->bhts', qc, kc, jnp.exp(rel))
        o_intra = jnp.einsum('bhts,bhsv->bhtv', A, ic)
        b_last = b[:, :, -1:, :]
        S = jnp.exp(b_last[:, :, 0, :, None]) * S + jnp.einsum(
            'bhsk,bhsv->bhkv', kc * jnp.exp(b_last - b), ic)
        return S, o_inter + o_intra

    S0 = jnp.zeros((B, H, Dk, Dv), jnp.float32)
    _, o = lax.scan(step, S0, xs)
    return o.transpose(1, 0, 3, 2, 4).reshape(B, T, H, Dv)


def causal_depthwise_conv(u, w, b):
    T = u.shape[1]
    up = jnp.pad(u, ((0, 0), (CONV_WIDTH - 1, 0), (0, 0)))
    y = b + w[0] * up[:, 0:T]
    for j in range(1, CONV_WIDTH):
        y = y + w[j] * up[:, j:j + T]
    return y


def setup_inputs(seed: int = 0) -> dict:
    key = jax.random.key(seed)
    ks = jax.random.split(key, 17)
    f32 = jnp.float32
    nrm = lambda k, shape, s: jax.random.normal(k, shape, f32) * s
    return {
        "x": nrm(ks[0], (BATCH, SEQ, D_MODEL), 1.0),
        "c": nrm(ks[1], (BATCH, D_MODEL), 1.0),
        "w_ada": nrm(ks[2], (DEPTH, D_MODEL, N_MOD * D_MODEL), 0.5 * D_MODEL ** -0.5),
        "b_ada": nrm(ks[3], (DEPTH, N_MOD * D_MODEL), 0.02),
        "g_mix_norm": 1.0 + nrm(ks[4], (DEPTH, D_MODEL), 0.02),
        "w_in": nrm(ks[5], (DEPTH, D_MODEL, IN_COLS), D_MODEL ** -0.5),
        "b_fox_f": nrm(ks[6], (DEPTH, FOX_HEADS), 0.1),
        "hgrn_lb_logits": nrm(ks[7], (DEPTH + 1, HGRN_KWIDTH), 1.0),
        "g_fox_out": 1.0 + nrm(ks[8], (DEPTH, FOX_WIDTH), 0.02),
        "g_hgrn_out": 1.0 + nrm(ks[9], (DEPTH, HGRN_VWIDTH), 0.02),
        "w_out": nrm(ks[10], (DEPTH, D_MIX, D_MODEL), D_MIX ** -0.5),
        "g_ffn_norm": 1.0 + nrm(ks[11], (DEPTH, D_MODEL), 0.02),
        "w_up": nrm(ks[12], (DEPTH, D_MODEL, 2 * D_FF), D_MODEL ** -0.5),
        "conv_w": nrm(ks[13], (DEPTH, CONV_WIDTH, 2 * D_FF), CONV_WIDTH ** -0.5),
        "conv_b": nrm(ks[14], (DEPTH, 2 * D_FF), 0.02),
        "w_down": nrm(ks[15], (DEPTH, D_FF, D_MODEL), D_FF ** -0.5),
        "g_final": 1.0 + nrm(ks[16], (D_MODEL,), 0.02),
    }


def reference(x, c, w_ada, b_ada, g_mix_norm, w_in, b_fox_f, hgrn_lb_logits, g_fox_out,
              g_hgrn_out, w_out, g_ffn_norm, w_up, conv_w, conv_b, w_down, g_final):
    B, T, _ = x.shape
    lb_table = jnp.cumsum(jax.nn.softmax(hgrn_lb_logits.astype(jnp.float32), axis=0), axis=0)
    cond = jax.nn.silu(c)
    for l in range(DEPTH):
        mod = cond @ w_ada[l] + b_ada[l]
        sh1, sc1, gt1, sh2, sc2, gt2 = jnp.split(mod, N_MOD, axis=-1)

        h = modulate(rms_norm(x, g_mix_norm[l]), sh1, sc1)
        proj = h @ w_in[l]
        fq = proj[..., FOX_Q0:FOX_K0].reshape(B, T, FOX_HEADS, FOX_HEAD_DIM)
        fk = proj[..., FOX_K0:FOX_V0].reshape(B, T, FOX_HEADS, FOX_HEAD_DIM)
        fv = proj[..., FOX_V0:FOX_F0].reshape(B, T, FOX_HEADS, FOX_HEAD_DIM)
        log_f = jax.nn.log_sigmoid((proj[..., FOX_F0:HG_Q0] + b_fox_f[l]).astype(jnp.float32))
        hq = proj[..., HG_Q0:HG_F0].reshape(B, T, HGRN_HEADS, HGRN_DK)
        hf = proj[..., HG_F0:HG_I0].reshape(B, T, HGRN_HEADS, HGRN_DK)
        hi = proj[..., HG_I0:HG_G0].reshape(B, T, HGRN_HEADS, HGRN_DV)
        hg = proj[..., HG_G0:IN_COLS].reshape(B, T, HGRN_HEADS, HGRN_DV)

        o_fox = forgetting_attention(fq, fk, fv, log_f)
        o_fox = rms_norm(o_fox, g_fox_out[l].reshape(FOX_HEADS, FOX_HEAD_DIM))
        o_hg = hgrn2_recurrence(hq, hf, hi, lb_table[l]).astype(x.dtype)
        o_hg = rms_norm(o_hg, g_hgrn_out[l].reshape(HGRN_HEADS, HGRN_DV)) * jax.nn.silu(hg)

        mixed = jnp.concatenate([o_fox.reshape(B, T, FOX_WIDTH),
                                 o_hg.reshape(B, T, HGRN_VWIDTH)], axis=-1)
        x = x + gt1[:, None, :] * (mixed @ w_out[l])

        h = modulate(rms_norm(x, g_ffn_norm[l]), sh2, sc2)
        u = causal_depthwise_conv(h @ w_up[l], conv_w[l], conv_b[l])
        a, v = jnp.split(u, 2, axis=-1)
        x = x + gt2[:, None, :] * ((jax.nn.silu(a) * v) @ w_down[l])
    return rms_norm(x, g_final)
```

```python
import bisect
from contextlib import ExitStack

import ml_dtypes
import numpy as np

import concourse.bass as bass
import concourse.mybir as mybir
from concourse.bass_utils import run_bass_kernel_spmd

F32 = mybir.dt.float32
BF16 = mybir.dt.bfloat16
AF = mybir.ActivationFunctionType
ALU = mybir.AluOpType

D = 4096
KC = D // 128
NCORES = 8
DFF = 11008
NFB = DFF // 128
EPS = 1e-6
QSCALE = 128.0 ** -0.5
NEG = -30000.0


class Sched:
    ENGS = ("pe", "dve", "act", "pool", "sp")

    def __init__(self, nc, es):
        self.nc = nc
        self.es = es
        self.eng = {"pe": nc.tensor, "dve": nc.vector, "act": nc.scalar,
                    "pool": nc.gpsimd, "sp": nc.sync}
        self.sem = {k: es.enter_context(nc.semaphore("S_" + k)) for k in self.ENGS}
        self.pos = {k: 0 for k in self.ENGS}
        self.last = {k: None for k in self.ENGS}
        self.sigpos = {k: [] for k in self.ENGS}
        self.sigval = {k: [] for k in self.ENGS}
        self.waited = {k: {} for k in self.ENGS}
        self.bufs = {}
        self.dsem = {}
        self.dpool = {}
        self.dnext = {}

    def _resolve(self, t):
        if t[0] == "d":
            return ("d", t[1]), self.dsem[t[1]][0], t[2]
        _, E, p = t
        i = bisect.bisect_left(self.sigpos[E], p)
        if i < len(self.sigpos[E]):
            return ("e", E), self.sem[E], self.sigval[E][i]
        self.last[E].then_inc(self.sem[E], 1)
        v = len(self.sigval[E]) + 1
        self.sigpos[E].append(self.pos[E] - 1)
        self.sigval[E].append(v)
        return ("e", E), self.sem[E], v

    def _wait(self, E, t):
        key, h, v = self._resolve(t)
        if self.waited[E].get(key, 0) < v:
            self.eng[E].wait_ge(h, v)
            self.waited[E][key] = v

    def _deps(self, E, reads, writes):
        deps = []
        for k in reads:
            b = self.bufs.get(k)
            if b and b[0] is not None:
                deps.append(("raw", b[0]))
        for k in writes:
            b = self.bufs.get(k)
            if b:
                if b[0] is not None:
                    deps.append(("waw", b[0]))
                for r in b[1]:
                    deps.append(("war", r))
        for kind, t in deps:
            if t[0] == "e" and t[1] == E:
                if E == "pe" or kind != "raw":
                    continue
            self._wait(E, t)

    def _record(self, t, reads, writes):
        for k in reads:
            b = self.bufs.setdefault(k, [None, []])
            b[1].append(t)
            if len(b[1]) > 96:
                b[1] = b[1][-96:]
        for k in writes:
            self.bufs[k] = [t, []]

    def op(self, E, fn, reads=(), writes=()):
        self._deps(E, reads, writes)
        ins = fn()
        self.last[E] = ins
        t = ("e", E, self.pos[E])
        self.pos[E] += 1
        self._record(t, reads, writes)
        return t

    NDSEM = {"sp": 44, "pool": 24, "act": 4}

    def dma(self, Q, sname, out, in_, reads=(), writes=(), **kw):
        if Q not in self.dpool:
            self.dpool[Q] = [[self.es.enter_context(self.nc.semaphore("D%s%d" % (Q, i))), 0]
                             for i in range(self.NDSEM[Q])]
            self.dnext[Q] = 0
            for i in range(self.NDSEM[Q]):
                self.dsem[(Q, i)] = self.dpool[Q][i]
        i = self.dnext[Q]
        self.dnext[Q] = (i + 1) % self.NDSEM[Q]
        d = self.dpool[Q][i]
        if d[1] > 0:
            self._wait(Q, ("d", (Q, i), d[1]))
        self._deps(Q, reads, writes)
        self.eng[Q].dma_start(out=out, in_=in_, **kw).then_inc(d[0], 16)
        d[1] += 16
        t = ("d", (Q, i), d[1])
        self._record(t, reads, writes)
        return t

    def wait_keys(self, E, keys):
        self._deps(E, list(keys), list(keys))

    def barrier(self):
        ts = [("e", E, self.pos[E] - 1) for E in self.ENGS if self.last[E] is not None]
        ts += [("d", n, c[1]) for n, c in self.dsem.items() if c[1] > 0]
        for C in self.ENGS:
            for t in ts:
                if t[0] == "e" and t[1] == C:
                    continue
                self._wait(C, t)
        self.bufs = {}


def _pieces(n, step=512):
    out = []
    a = 0
    while a < n:
        out.append((a, min(step, n - a)))
        a += step
    return out


import os
KCUT = int(os.environ.get("KCUT", "0"))


class _Cut(Exception):
    pass


NEED = {
    "A": {"ccol", "wada", "bada", "identb", "identf"},
    "B": {"modall_in", "gmix", "x", "wfox", "whg", "nbfox", "lbl", "gfox", "ghg", "identb", "identf", "trif", "fmask",
          "hmask", "cmask", "rowmask"},
    "C": {"modall_in", "gffn", "xo", "gfin", "wout", "wup", "wdown", "convw", "convb", "identb", "identf", "hflag", "mixin"},
}


def build_program(T, upto="all", dbg=False, phase="A"):
    try:
        return _build_program(T, upto, dbg, phase)
    except _Cut as e:
        return e.args[0]


def _build_program(T, upto="all", dbg=False, phase="A"):
    NT = T // NCORES
    NTH = NT + 2
    HB = 64
    NTP = HB + NT
    NTT = T // 128
    NST = T // 512
    nc = bass.Bass("TRN2", target_bir_lowering=False)
    okind = "ExternalOutput" if dbg else "Internal"

    def din(name, shape, dt=F32):
        if name not in NEED[phase]:
            return None
        return nc.dram_tensor(name, list(shape), dt, kind="ExternalInput").ap()

    def dscr(name, shape, dt=F32, out=False):
        if out:
            return nc.dram_tensor(name, list(shape), dt, kind="ExternalOutput").ap()
        return nc.dram_tensor(name, list(shape), dt).ap()

    x = din("x", [T, D])
    xo = din("xo", [NTH, D])
    ccol = din("ccol", [128, KC])
    gmix = din("gmix", [128, KC])
    gffn = din("gffn", [128, KC])
    gfin = din("gfin", [1, D])
    wada = din("wada", [D, 3072])
    bada = din("bada", [1, 3072])
    wfox = din("wfox", [D, 770])
    whg = din("whg", [D, 1024])
    nbfox = din("nbfox", [128, 2])
    lbl = din("lbl", [128, 4])
    gfox = din("gfox", [128, 2])
    ghg = din("ghg", [128, 2])
    order = ["p0", "n", "f", "h", "x", "o1", "o2", "u", "d", "all"]
    lim = order.index(upto)
    wout = din("wout", [D, D])
    wup = din("wup", [D, 2 * DFF])
    wdown = din("wdown", [DFF, D])
    convw = din("convw", [128, 2 * NFB, 3])
    convb = din("convb", [128, 2 * NFB])
    identb_d = din("identb", [128, 128], BF16)
    identf_d = din("identf", [128, 128])
    trif_d = din("trif", [128, 128])
    fmask_d = din("fmask", [128, 128])
    hmask_d = din("hmask", [128, 512], BF16)
    cmask_d = din("cmask", [128, 512])
    rowmask_d = din("rowmask", [128, 4])
    sel_d = din("sel", [128, 8])
    hflag_d = din("hflag", [128, 1])

    out = dscr("out", [NT, D], out=True) if phase == "C" else None
    modsrc = dscr("modsrc", [1, 3072], out=True) if phase == "A" else None
    modall = din("modall_in", [8, 3072])
    mixin = din("mixin", [D, NTH], BF16)
    dbg_mod = None
    dbg_mix = None
    dbg = False
    hT_d = dscr("hT_d", [D, T], BF16, out=dbg)
    cum_d = dscr("cum_d", [2 * NTT, 128])
    qT_d = dscr("qT_d", [256, T], BF16)
    mix_src = dscr("mix_src", [512, T], BF16, out=True) if phase == "B" else None
    x1_d = dscr("x1_d", [NTH, D], out=dbg)
    g_d = dscr("g_d", [NT // 128, 128, NFB, 128], BF16, out=dbg)
    x2_d = dscr("x2_d", [NT, D])

    PH = {"p0": "A", "n": "B", "f": "B", "h": "B", "x": "-", "o1": "C", "o2": "C", "u": "C", "d": "C", "all": "C"}

    def active(st):
        return PH[st] == phase

    with ExitStack() as es:
        S = Sched(nc, es)
        V, A, PE, PO = nc.vector, nc.scalar, nc.tensor, nc.gpsimd

        def sb(st, name, shape, dt=F32):
            return st.enter_context(nc.sbuf_tensor("sb_" + name, list(shape), dt))

        def pst(st, name, shape, dt=F32):
            return st.enter_context(nc.psum_tensor("ps_" + name, list(shape), dt))

        identb = sb(es, "identb", [128, 128], BF16)
        identf = sb(es, "identf", [128, 128])
        onesb = sb(es, "onesb", [128, 128], BF16)
        onesf = sb(es, "onesf", [128, 128])
        modcol = sb(es, "modcol", [128, 6 * KC])
        gs1 = sb(es, "gs1", [128, KC])
        gs2 = sb(es, "gs2", [128, KC])
        cfl = sb(es, "cfl", [1, 16])
        S.dma("sp", "c0", identb[:], identb_d[:, :], writes=["identb"])
        S.dma("sp", "c0", identf[:], identf_d[:, :], writes=["identf"])
        S.op("pool", lambda: PO.memset(onesb[:], 1.0), writes=["onesb"])
        S.op("pool", lambda: PO.memset(onesf[:], 1.0), writes=["onesf"])

        if phase == "A":
          with ExitStack() as st:
              cc = sb(st, "cc", [128, KC])
              cond = sb(st, "cond", [128, KC])
              bar = sb(st, "bar", [1, 3072])
              modrow = sb(st, "modrow", [1, 3072])
              wa = [sb(st, f"wa{i}", [128, 3072]) for i in range(3)]
              mt = [sb(st, f"mt{i}", [128, 128]) for i in range(2)]
              gtmp = sb(st, "gtmp", [128, KC])
              ps = [pst(st, f"p0ps{i}", [128, 512]) for i in range(6)]
              pt = pst(st, "p0pt", [128, 512])
              S.dma("sp", "p0a", cc[:], ccol[:, :], writes=["cc"])
              S.dma("sp", "p0a", bar[:], bada[:, :], writes=["bar"])
              S.op("act", lambda: A.activation(out=cond[:], in_=cc[:], func=AF.Silu), reads=["cc"], writes=["cond"])
              for kc in range(KC):
                  sl = kc % 3
                  S.dma("sp", f"wa{sl}", wa[sl][:], wada[kc * 128:(kc + 1) * 128, :], writes=[("wa", sl)])
                  for g in range(6):
                      S.op("pe", lambda: PE.matmul(ps[g][0:1, :], lhsT=cond[:, kc:kc + 1],
                                                   rhs=wa[sl][:, g * 512:(g + 1) * 512],
                                                   start=(kc == 0), stop=(kc == KC - 1)),
                           reads=[("wa", sl), "cond"], writes=[("p0ps", g)])
              for g in range(6):
                  S.op("dve", lambda: V.tensor_tensor(out=modrow[0:1, g * 512:(g + 1) * 512], in0=ps[g][0:1, :],
                                                      in1=bar[0:1, g * 512:(g + 1) * 512], op=ALU.add),
                       reads=[("p0ps", g), "bar"], writes=["modrow"])
              S.dma("sp", "p0b", modsrc[:, :], modrow[:], reads=["modrow"], writes=["modsrc"])
              S.barrier()
        if phase in ("B", "C"):
          with ExitStack() as st:
            mt = [sb(st, f"mt{i}", [128, 128]) for i in range(2)]
            gtmp = sb(st, "gtmp", [128, KC])
            pt = pst(st, "p0pt", [128, 512])
            modv = modall.rearrange("r (a p) -> (r a) p", p=128)
            for i, (r0, nr) in enumerate(((0, 128), (128, 64))):
                S.dma("sp", "p0c", mt[i][0:nr, :], modv[r0:r0 + nr, :], writes=[("mt", i)])
                S.op("pe", lambda: PE.transpose(pt[:, r0:r0 + nr], mt[i][0:nr, :], identf[0:nr, 0:nr]),
                     reads=[("mt", i), "identf"], writes=["p0pt"])
            S.op("dve", lambda: V.tensor_copy(out=modcol[:], in_=pt[:, 0:192]), reads=["p0pt"], writes=["modcol"])
            for (gcol_d, gs, m) in (((gmix, gs1, 1),) if phase == "B" else ((gffn, gs2, 4),)):
                S.dma("sp", "p0d", gtmp[:], gcol_d[:, :], writes=["gtmp"])
                S.op("dve", lambda: V.scalar_tensor_tensor(out=gs[:], in0=modcol[:, m * KC:(m + 1) * KC], scalar=1.0,
                                                           in1=gtmp[:], op0=ALU.add, op1=ALU.mult),
                     reads=["modcol", "gtmp"], writes=[("gs", m)])
            S.barrier()
        sh1 = modcol[:, 0 * KC:1 * KC]
        sh2 = modcol[:, 3 * KC:4 * KC]
        modflat = modall.rearrange("r (o f) -> o (r f)", o=1) if modall is not None else None

        def norm_transpose(st_keys, xt, ntk, gs, sh, dst_fn, stg_, slot, bufs):
            junk, ss, rt, rstd, xn, ptp = bufs
            tag = "%s%d_" % (stg_, slot)
            S.op("act", lambda: A.activation(out=junk[0:ntk, :], in_=xt[0:ntk, :], func=AF.Square,
                                             accum_out=ss[0:ntk, 0:1]),
                 reads=st_keys, writes=[stg_ + "junk", tag + "ss"])
            S.op("act", lambda: A.activation(out=rt[0:ntk, :], in_=ss[0:ntk, :], func=AF.Sqrt,
                                             scale=1.0 / D, bias=epsb[0:ntk, :]),
                 reads=[tag + "ss"], writes=[tag + "rt"])
            S.op("dve", lambda: V.reciprocal(out=rstd[0:ntk, :], in_=rt[0:ntk, :]), reads=[tag + "rt"],
                 writes=[tag + "rstd"])
            S.op("dve", lambda: V.tensor_scalar(out=xn[0:ntk, :], in0=xt[0:ntk, :], scalar1=rstd[0:ntk, 0:1],
                                                scalar2=None, op0=ALU.mult),
                 reads=st_keys + [tag + "rstd"], writes=[tag + "xn"])
            for half in range(2):
                for c16 in range(16):
                    c = half * 16 + c16
                    S.op("pe", lambda: PE.transpose(ptp[half][:, c16, 0:ntk], xn[0:ntk, c * 128:(c + 1) * 128],
                                                    identb[0:ntk, 0:ntk]),
                         reads=[tag + "xn", "identb"], writes=[(stg_ + "ptp", half)])
                for c16 in range(16):
                    c = half * 16 + c16
                    dst, wkeys = dst_fn(c)
                    if c16 < 8:
                        S.op("act", lambda: A.activation(out=dst, in_=ptp[half][:, c16, 0:ntk], func=AF.Identity,
                                                         scale=gs[:, c:c + 1], bias=sh[:, c:c + 1]),
                             reads=[(stg_ + "ptp", half)], writes=wkeys)
                    else:
                        S.op("dve", lambda: V.tensor_scalar(out=dst, in0=ptp[half][:, c16, 0:ntk],
                                                            scalar1=gs[:, c:c + 1], scalar2=sh[:, c:c + 1],
                                                            op0=ALU.mult, op1=ALU.add),
                             reads=[(stg_ + "ptp", half)], writes=wkeys)

        wl_state = {"n": 0}

        def wload(stgs, dst, src, a, b, wkey):
            i = wl_state["n"]
            wl_state["n"] += 1
            sl = i % len(stgs)
            view = stgs[sl][:, 0:a * b].rearrange("p (a b) -> p a b", b=b)
            S.dma("sp", "w", view, src, writes=[("wstg", sl)])
            if i % 2 == 0:
                S.op("pool", lambda: PO.tensor_copy(out=dst, in_=view), reads=[("wstg", sl)], writes=[wkey])
            else:
                S.op("act", lambda: A.copy(out=dst, in_=view), reads=[("wstg", sl)], writes=[wkey])

        epsb = sb(es, "epsb", [128, 1])
        S.op("pool", lambda: PO.memset(epsb[:], EPS), writes=["epsb"])
        S.barrier()

        if active("n"):
            with ExitStack() as st:
                xts = [sb(st, f"n_xt{i}", [128, D]) for i in range(2)]
                junk = sb(st, "n_junk", [128, D], BF16)
                xns = [sb(st, f"n_xn{i}", [128, D], BF16) for i in range(2)]
                hss = [sb(st, f"n_hs{i}", [128, KC, 512], BF16) for i in range(2)]
                ss = [sb(st, f"n_ss{i}", [128, 1]) for i in range(2)]
                rt = [sb(st, f"n_rt{i}", [128, 1]) for i in range(2)]
                rstd = [sb(st, f"n_rstd{i}", [128, 1]) for i in range(2)]
                ptp = [pst(st, f"n_ptp{i}", [128, 16, 128], BF16) for i in range(2)]
                hTv = hT_d.rearrange("(c p) t -> p c t", p=128)
                for sti in range(NST):
                    hs = hss[sti % 2]
                    for q in range(4):
                        tt = sti * 4 + q
                        xt = xts[tt % 2]
                        S.dma("sp", f"n_x{tt % 2}", xt[:], x[tt * 128:(tt + 1) * 128, :], writes=[("n_xt", tt % 2)])
                        norm_transpose([("n_xt", tt % 2)], xt, 128, gs1, sh1,
                                       lambda c: (hs[:, c, q * 128:(q + 1) * 128], [("n_hs", sti % 2)]),
                                       "n", tt % 2, (junk, ss[tt % 2], rt[tt % 2], rstd[tt % 2], xns[tt % 2], ptp))
                    S.dma("sp", f"n_h{sti % 2}", hTv[:, :, sti * 512:(sti + 1) * 512], hs[:],
                          reads=[("n_hs", sti % 2)], writes=["hT_d"])
                S.barrier()

        if active("f") and os.environ.get("KSKIPF") != "1":
            with ExitStack() as st:
                qtile = [sb(st, f"f_qt{h}", [128, 256], BF16) for h in range(2)]
                KT = [sb(st, f"f_KT{h}", [128, T], BF16) for h in range(2)]
                Vb = sb(st, "f_Vb", [128, NTT, 256], BF16)
                fl = sb(st, "f_fl", [128, 2, NTT])
                cneg = sb(st, "f_cneg", [128, 2, NTT])
                nb = sb(st, "f_nb", [128, 2])
                gf = sb(st, "f_gf", [128, 2])
                fmask = sb(st, "f_fmask", [128, 128])
                trif = sb(st, "f_trif", [128, 128])
                S.dma("sp", "c0", nb[:], nbfox[:, :], writes=["f_nb"])
                S.dma("sp", "c0", gf[:], gfox[:, :], writes=["f_gf"])
                S.dma("sp", "c0", fmask[:], fmask_d[:, :], writes=["f_fmask"])
                S.dma("sp", "c0", trif[:], trif_d[:, :], writes=["f_trif"])
                hTv = hT_d.rearrange("(c p) t -> p c t", p=128)
                with ExitStack() as st2:
                    Wf = sb(st2, "f_W", [128, KC, 832], BF16)
                    psF = [pst(st2, f"f_ps{i}", [128, 512]) for i in range(2)]
                    psV = [pst(st2, f"f_pv{i}", [128, 512]) for i in range(2)]
                    psf = pst(st2, "f_pf", [128, 512])
                    wv = wfox.rearrange("(c p) n -> p c n", p=128)
                    with ExitStack() as stw:
                        stgs = [sb(stw, f"f_stg{i}", [128, 2048]) for i in range(3)]
                        for i in range(16):
                            wload(stgs, Wf[:, i * 2:(i + 1) * 2, 0:770], wv[:, i * 2:(i + 1) * 2, :], 2, 770, "f_W")
                        S.barrier()
                    hsl = [sb(st2, f"f_hs{i}", [128, KC, 256], BF16) for i in range(1)]
                    for ht in range(T // 256 if KCUT != 1 else 0):
                        hs = hsl[0]
                        hk = ("f_hs", 0)
                        for i in range(4):
                            S.dma("sp", "f_h", hs[:, i * 8:(i + 1) * 8, :], hTv[:, i * 8:(i + 1) * 8, ht * 256:(ht + 1) * 256],
                                  reads=["hT_d"], writes=[hk])
                        tok = slice(ht * 256, (ht + 1) * 256)
                        KFP = os.environ.get("KF_PARTS", "qkvf")
                        for ct in range(4):
                            if ("q" not in KFP and ct < 2) or ("k" not in KFP and ct >= 2):
                                continue
                            pb = psF[ct % 2]
                            for c in range(KC):
                                S.op("pe", lambda: PE.matmul(pb[:, 0:256], lhsT=Wf[:, c, ct * 128:(ct + 1) * 128],
                                                             rhs=hs[:, c, :], start=(c == 0), stop=(c == KC - 1)),
                                     reads=["f_W", hk], writes=[("f_ps", ct % 2)])
                            if ct < 2:
                                S.op("act", lambda: A.activation(out=qtile[ct][:], in_=pb[:, 0:256], func=AF.Copy,
                                                                 scale=QSCALE),
                                     reads=[("f_ps", ct % 2)], writes=[("f_qt", ct)])
                                S.dma("sp", "f_q", qT_d[ct * 128:(ct + 1) * 128, tok], qtile[ct][:], reads=[("f_qt", ct)],
                                      writes=["qT_d"])
                            else:
                                S.op("dve", lambda: V.tensor_copy(out=KT[ct - 2][:, tok], in_=pb[:, 0:256]),
                                     reads=[("f_ps", ct % 2)], writes=[("f_KT", ct - 2)])
                        for tq in range(2):
                            tti = ht * 2 + tq
                            pb = psV[tq]
                            for c in range(KC if "v" in KFP else 0):
                                S.op("pe", lambda: PE.matmul(pb[:, 0:256], lhsT=hs[:, c, tq * 128:(tq + 1) * 128],
                                                             rhs=Wf[:, c, 512:768], start=(c == 0), stop=(c == KC - 1)),
                                     reads=["f_W", hk], writes=[("f_pv", tq)])
                            S.op("act", lambda: A.copy(out=Vb[:, tti, :], in_=pb[:, 0:256]),
                                 reads=[("f_pv", tq)], writes=["f_Vb"])
                            for c in range(KC if "f" in KFP else 0):
                                S.op("pe", lambda: PE.matmul(psf[:, tq * 2:tq * 2 + 2],
                                                             lhsT=hs[:, c, tq * 128:(tq + 1) * 128],
                                                             rhs=Wf[:, c, 768:770], start=(c == 0), stop=(c == KC - 1)),
                                     reads=["f_W", hk], writes=["f_pf"])
                            S.op("dve", lambda: V.tensor_copy(out=fl[:, :, tti], in_=psf[:, tq * 2:tq * 2 + 2]),
                                 reads=["f_pf"], writes=["f_fl"])
                    for h in range(2 if KCUT not in (1, 2) else 0):
                        S.op("act", lambda: A.activation(out=fl[:, h, :], in_=fl[:, h, :], func=AF.Exp, scale=-1.0,
                                                         bias=nb[:, h:h + 1]),
                             reads=["f_fl", "f_nb"], writes=["f_fl"])
                    if KCUT in (1, 2):
                        S.barrier()
                        raise _Cut(nc)
                    S.op("act", lambda: A.activation(out=fl[:], in_=fl[:], func=AF.Ln, bias=onesf[:, 0:1]),
                         reads=["f_fl", "onesf"], writes=["f_fl"])
                    flv = fl[:].rearrange("p h j -> p (h j)")
                    S.op("pe", lambda: PE.matmul(psF[0][:, 0:2 * NTT], lhsT=trif[:], rhs=flv, start=True, stop=True),
                         reads=["f_fl", "f_trif"], writes=[("f_ps", 0)])
                    S.op("pe", lambda: PE.matmul(psF[1][:, 0:2 * NTT], lhsT=onesf[:], rhs=flv, start=True, stop=True),
                         reads=["f_fl", "onesf"], writes=[("f_ps", 1)])
                    tot = sb(st2, "f_tot", [128, 2, NTT])
                    inc = sb(st2, "f_inc", [128, 2, NTT])
                    S.op("dve", lambda: V.tensor_copy(out=tot[:].rearrange("p h j -> p (h j)"), in_=psF[1][:, 0:2 * NTT]),
                         reads=[("f_ps", 1)], writes=["f_tot"])
                    for h in range(2):
                        S.op("dve", lambda: V.tensor_tensor_scan(out=inc[:, h, :], data0=onesf[:, 0:NTT],
                                                                 data1=tot[:, h, :], initial=0.0,
                                                                 op0=ALU.mult, op1=ALU.add),
                             reads=["f_tot", "onesf"], writes=["f_inc"])
                    S.op("dve", lambda: V.tensor_tensor(out=inc[:], in0=inc[:], in1=tot[:], op=ALU.subtract),
                         reads=["f_inc", "f_tot"], writes=["f_inc"])
                    S.op("dve", lambda: V.tensor_tensor(out=cneg[:].rearrange("p h j -> p (h j)"), in0=psF[0][:, 0:2 * NTT],
                                                        in1=inc[:].rearrange("p h j -> p (h j)"), op=ALU.add),
                         reads=[("f_ps", 0), "f_inc"], writes=["f_cneg"])
                    S.op("pe", lambda: PE.transpose(psV[0][0:2 * NTT, 0:128], cneg[:].rearrange("p h j -> p (h j)"),
                                                    identf[:]),
                         reads=["f_cneg", "identf"], writes=[("f_pv", 0)])
                    cT = sb(st2, "f_cT", [128, 128])
                    S.op("dve", lambda: V.tensor_copy(out=cT[0:2 * NTT, :], in_=psV[0][0:2 * NTT, 0:128]),
                         reads=[("f_pv", 0)], writes=["f_cT"])
                    S.dma("sp", "f_c", cum_d[:, :], cT[0:2 * NTT, :], reads=["f_cT"], writes=["cum_d"])
                    S.barrier()
                if KCUT == 3:
                    S.barrier()
                    raise _Cut(nc)
                with ExitStack() as st2:
                    cq = [sb(st2, f"a_cq{i}", [128, 512]) for i in range(2)]
                    qblk = [sb(st2, f"a_qb{i}", [128, 512], BF16) for i in range(2)]
                    Lb = [sb(st2, f"a_L{i}", [128, 512]) for i in range(3)]
                    PT = [sb(st2, f"a_P{i}", [128, 512], BF16) for i in range(3)]
                    rD = sb(st2, "a_rD", [128, 512]); o = sb(st2, "a_o", [128, 512])
                    sq = sb(st2, "a_sq", [128, 512], BF16); rt2 = sb(st2, "a_rt", [128, 512])
                    mx = [sb(st2, f"a_mx{i}", [128, 512], BF16) for i in range(2)]
                    pS = [pst(st2, f"a_ps{i}", [128, 512]) for i in range(3)]
                    pO = pst(st2, "a_po", [128, 512]); pD = pst(st2, "a_pd", [128, 512]); pN = pst(st2, "a_pn", [128, 512])
                    n = 0
                    nq = 0
                    for h in range(2):
                        for qb in range(NST):
                            cqt = cq[nq % 2]
                            src = cum_d[h * NTT + qb * 4:h * NTT + qb * 4 + 4, :].rearrange("(o j) p -> o (j p)", o=1)
                            S.dma("sp", f"a_cq{nq % 2}", cqt[:], src.partition_broadcast(128), reads=["cum_d"],
                                  writes=[("a_cq", nq % 2)])
                            qbt = qblk[nq % 2]
                            S.dma("sp", "a_q", qbt[:], qT_d[h * 128:(h + 1) * 128, qb * 512:(qb + 1) * 512], reads=["qT_d"],
                                  writes=[("a_qb", nq % 2)])
                            nkt = 4 * qb + 4
                            for kt in range(nkt):
                                m = kt - 4 * qb
                                c0 = 0 if m < 0 else 128 * m
                                cols = slice(c0, 512)
                                s3 = n % 3
                                S.op("pe", lambda: PE.matmul(pS[s3][:, cols], lhsT=KT[h][:, kt * 128:(kt + 1) * 128],
                                                             rhs=qbt[:, c0:512], start=True, stop=True),
                                     reads=[("f_KT", h), ("a_qb", nq % 2)], writes=[("a_ps", s3)])
                                S.op("dve", lambda: V.tensor_tensor(out=Lb[s3][:, cols], in0=pS[s3][:, cols],
                                                                    in1=cqt[:, cols], op=ALU.subtract),
                                     reads=[("a_ps", s3), ("a_cq", nq % 2)], writes=[("a_L", s3)])
                                if m >= 0:
                                    S.op("dve", lambda: V.tensor_tensor(out=Lb[s3][:, c0:c0 + 128], in0=Lb[s3][:, c0:c0 + 128],
                                                                        in1=fmask[:], op=ALU.add),
                                         reads=[("a_L", s3), "f_fmask"], writes=[("a_L", s3)])
                                S.op("act", lambda: A.activation(out=PT[s3][:, cols], in_=Lb[s3][:, cols], func=AF.Exp,
                                                                 bias=cneg[:, h, kt:kt + 1]),
                                     reads=[("a_L", s3), "f_cneg"], writes=[("a_P", s3)])
                                S.op("pe", lambda: PE.matmul(pO[:, cols], lhsT=Vb[:, kt, h * 128:(h + 1) * 128], rhs=PT[s3][:, cols],
                                                             start=(kt == 0), stop=(kt == nkt - 1), skip_group_check=True),
                                     reads=[("a_P", s3), "f_Vb"], writes=["a_po"])
                                S.op("pe", lambda: PE.matmul(pD[:, cols], lhsT=onesb[:], rhs=PT[s3][:, cols],
                                                             start=(kt == 0), stop=(kt == nkt - 1), skip_group_check=True),
                                     reads=[("a_P", s3), "onesb"], writes=["a_pd"])
                                n += 1
                            S.op("dve", lambda: V.reciprocal(out=rD[:], in_=pD[:]), reads=["a_pd"], writes=["a_rD"])
                            S.op("dve", lambda: V.tensor_tensor(out=o[:], in0=pO[:], in1=rD[:], op=ALU.mult),
                                 reads=["a_po", "a_rD"], writes=["a_o"])
                            S.op("act", lambda: A.activation(out=sq[:], in_=o[:], func=AF.Square), reads=["a_o"], writes=["a_sq"])
                            S.op("pe", lambda: PE.matmul(pN[:], lhsT=onesb[:], rhs=sq[:], start=True, stop=True),
                                 reads=["a_sq", "onesb"], writes=["a_pn"])
                            S.op("act", lambda: A.activation(out=rt2[:], in_=pN[:], func=AF.Sqrt, scale=1.0 / 128, bias=epsb[:]),
                                 reads=["a_pn", "epsb"], writes=["a_rt"])
                            S.op("dve", lambda: V.reciprocal(out=rt2[:], in_=rt2[:]), reads=["a_rt"], writes=["a_rt"])
                            S.op("dve", lambda: V.tensor_tensor(out=o[:], in0=o[:], in1=rt2[:], op=ALU.mult),
                                 reads=["a_o", "a_rt"], writes=["a_o"])
                            mxt = mx[nq % 2]
                            S.op("act", lambda: A.activation(out=mxt[:], in_=o[:], func=AF.Copy, scale=gf[:, h:h + 1]),
                                 reads=["a_o", "f_gf"], writes=[("a_mx", nq % 2)])
                            S.dma("sp", f"a_mx{nq % 2}", mix_src[h * 128:(h + 1) * 128, qb * 512:(qb + 1) * 512], mxt[:],
                                  reads=[("a_mx", nq % 2)], writes=["mix_src"])
                            nq += 1
                    S.barrier()
                S.barrier()

        if active("h"):
            with ExitStack() as st:
                Wh = sb(st, "h_W", [128, KC, 1024], BF16)
                wv = whg.rearrange("(c p) n -> p c n", p=128)
                with ExitStack() as stw:
                    stgs = [sb(stw, f"h_stg{i}", [128, 2048]) for i in range(3)]
                    for i in range(16):
                        wload(stgs, Wh[:, i * 2:(i + 1) * 2, :], wv[:, i * 2:(i + 1) * 2, :], 2, 1024, "h_W")
                    S.barrier()
                hsl = [sb(st, f"h_hs{i}", [128, KC, 512], BF16) for i in range(1)]
                lbt = sb(st, "h_lbl", [128, 4]); lb = sb(st, "h_lb", [128, 2]); oml = sb(st, "h_oml", [128, 2])
                gh = sb(st, "h_gh", [128, 2]); hmask = sb(st, "h_hmask", [128, 512], BF16)
                cmask = sb(st, "h_cmask", [128, 512]); rowmask = sb(st, "h_rowmask", [128, 4])
                qs = sb(st, "h_qs", [128, 512]); sg = sb(st, "h_sg", [128, 512]); fg = sb(st, "h_fg", [128, 512])
                lf = sb(st, "h_lf", [128, 512]); kk = sb(st, "h_kk", [128, 512]); bb = sb(st, "h_b", [128, 512])
                eb = sb(st, "h_eb", [128, 512]); enb = sb(st, "h_enb", [128, 512])
                sgt = [sb(st, f"h_sgt{h}", [128, 512]) for h in range(2)]
                qbT = [sb(st, f"h_qbT{h}", [128, 512], BF16) for h in range(2)]
                kdT = [sb(st, f"h_kdT{h}", [128, 512], BF16) for h in range(2)]
                ebl = [sb(st, f"h_ebl{h}", [128, 16]) for h in range(2)]
                itok = sb(st, "h_itok", [128, 4, 256], BF16)
                kdm = [sb(st, f"h_kdm{h}", [128, 4, 4, 128], BF16) for h in range(2)]
                AmT = [sb(st, f"h_AmT{h}", [128, 512], BF16) for h in range(2)]
                Sb = [[sb(st, f"h_S{h}{i}", [128, 128], BF16) for i in range(2)] for h in range(2)]
                oo = sb(st, "h_o", [128, 512]); sq = sb(st, "h_sq", [128, 512], BF16); rt2 = sb(st, "h_rt", [128, 512])
                mx = [sb(st, f"h_mx{i}", [128, 512], BF16) for i in range(2)]
                psP = [pst(st, f"h_pp{i}", [128, 512]) for i in range(2)]
                psT8 = pst(st, "h_pt", [128, 8, 128], BF16)
                psT = psT8[:, 0:4, :]
                psA = psP[0]
                po = [pst(st, f"h_po{h}", [128, 512]) for h in range(2)]
                psS = [pst(st, f"h_pss{h}", [128, 512]) for h in range(2)]
                S.dma("sp", "c0", lbt[:], lbl[:, :], writes=["h_lbl"])
                S.dma("sp", "c0", gh[:], ghg[:, :], writes=["h_gh"])
                S.dma("sp", "c0", hmask[:], hmask_d[:, :], writes=["h_hmask"])
                S.dma("sp", "c0", cmask[:], cmask_d[:, :], writes=["h_cmask"])
                S.dma("sp", "c0", rowmask[:], rowmask_d[:, :], writes=["h_rowmask"])
                S.op("dve", lambda: V.tensor_tensor(out=lb[:], in0=lbt[:, 0:2], in1=lbt[:, 2:4], op=ALU.subtract),
                     reads=["h_lbl"], writes=["h_lb"])
                S.op("act", lambda: A.activation(out=lb[:], in_=lb[:], func=AF.Sigmoid), reads=["h_lb"], writes=["h_lb"])
                S.op("dve", lambda: V.tensor_scalar(out=oml[:], in0=lb[:], scalar1=-1.0, scalar2=1.0, op0=ALU.mult, op1=ALU.add),
                     reads=["h_lb"], writes=["h_oml"])
                for h in range(2):
                    S.op("pool", lambda: PO.memset(Sb[h][0][:], 0.0), writes=[("h_S", h, 0)])
                hTv = hT_d.rearrange("(c p) t -> p c t", p=128)
                npp = 0
                for sti in range(NST if os.environ.get("KSKIPH") != "1" else 0):
                    hs = hsl[0]
                    hk = ("h_hs", 0)
                    S.dma("sp", "h_h", hs[:], hTv[:, :, sti * 512:(sti + 1) * 512], reads=["hT_d"], writes=[hk])

                    def proj(col0):
                        nonlocal npp
                        pb = psP[npp % 2]
                        key = ("h_pp", npp % 2)
                        npp += 1
                        for c in range(KC):
                            S.op("pe", lambda: PE.matmul(pb[:], lhsT=Wh[:, c, col0:col0 + 128], rhs=hs[:, c, :],
                                                         start=(c == 0), stop=(c == KC - 1)),
                                 reads=["h_W", hk], writes=[key])
                        return pb, key

                    for h in range(2):
                        pb, key = proj(h * 128)
                        S.op("act", lambda: A.activation(out=qs[:], in_=pb[:], func=AF.Silu), reads=[key], writes=["h_qs"])
                        pb, key = proj(256 + h * 128)
                        S.op("act", lambda: A.activation(out=sg[:], in_=pb[:], func=AF.Sigmoid), reads=[key], writes=["h_sg"])
                        pb, key = proj(512 + h * 128)
                        S.op("act", lambda: A.activation(out=sgt[h][:], in_=pb[:], func=AF.Silu), reads=[key],
                             writes=[("h_sgt", h)])
                        S.op("dve", lambda: V.tensor_scalar(out=fg[:], in0=sg[:], scalar1=oml[:, h:h + 1],
                                                            scalar2=lb[:, h:h + 1], op0=ALU.mult, op1=ALU.add),
                             reads=["h_sg", "h_oml", "h_lb"], writes=["h_fg"])
                        S.op("act", lambda: A.activation(out=lf[:], in_=fg[:], func=AF.Ln), reads=["h_fg"], writes=["h_lf"])
                        S.op("dve", lambda: V.tensor_scalar(out=kk[:], in0=fg[:], scalar1=-1.0, scalar2=1.0,
                                                            op0=ALU.mult, op1=ALU.add),
                             reads=["h_fg"], writes=["h_kk"])
                        S.op("dve", lambda: V.tensor_tensor_scan(out=bb[:], data0=cmask[:], data1=lf[:], initial=0.0,
                                                                 op0=ALU.mult, op1=ALU.add),
                             reads=["h_lf", "h_cmask"], writes=["h_b"])
                        S.op("act", lambda: A.activation(out=eb[:], in_=bb[:], func=AF.Exp), reads=["h_b"], writes=["h_eb"])
                        S.op("act", lambda: A.activation(out=enb[:], in_=bb[:], func=AF.Exp, scale=-1.0), reads=["h_b"],
                             writes=["h_enb"])
                        S.op("dve", lambda: V.tensor_tensor(out=qbT[h][:], in0=qs[:], in1=eb[:], op=ALU.mult),
                             reads=["h_qs", "h_eb"], writes=[("h_qbT", h)])
                        S.op("dve", lambda: V.tensor_tensor(out=kdT[h][:], in0=kk[:], in1=enb[:], op=ALU.mult),
                             reads=["h_kk", "h_enb"], writes=[("h_kdT", h)])
                        S.op("dve", lambda: V.tensor_copy(out=ebl[h][:], in_=eb[:].rearrange("p (j s) -> p j s", s=32)[:, :, 31]),
                             reads=["h_eb"], writes=[("h_ebl", h)])
                    for tq in range(4):
                        pb = psP[npp % 2]
                        key = ("h_pp", npp % 2)
                        npp += 1
                        for c in range(KC):
                            S.op("pe", lambda: PE.matmul(pb[:, 0:256], lhsT=hs[:, c, tq * 128:(tq + 1) * 128],
                                                         rhs=Wh[:, c, 768:1024], start=(c == 0), stop=(c == KC - 1)),
                                 reads=["h_W", hk], writes=[key])
                        S.op("act" if tq % 2 else "dve",
                             (lambda: A.copy(out=itok[:, tq, :], in_=pb[:, 0:256])) if tq % 2 else
                             (lambda: V.tensor_copy(out=itok[:, tq, :], in_=pb[:, 0:256])),
                             reads=[key], writes=["h_itok"])
                    for h in range(2):
                        for tq in range(4):
                            S.op("pe", lambda: PE.transpose(psT[:, tq, :], kdT[h][:, tq * 128:(tq + 1) * 128], identb[:]),
                                 reads=[("h_kdT", h), "identb"], writes=["h_pt"])
                        for jj in range(4):
                            if True:
                                S.op("dve", lambda: V.tensor_scalar(out=kdm[h][:, :, jj, :], in0=psT,
                                                                    scalar1=rowmask[:, jj:jj + 1], scalar2=None, op0=ALU.mult),
                                     reads=["h_pt", "h_rowmask"], writes=[("h_kdm", h)])
                            else:
                                S.op("act", lambda: A.activation(out=kdm[h][:, :, jj, :], in_=psT, func=AF.Copy,
                                                                 scale=rowmask[:, jj:jj + 1]),
                                     reads=["h_pt", "h_rowmask"], writes=[("h_kdm", h)])
                        for tq in range(4):
                            cs = slice(tq * 128, (tq + 1) * 128)
                            S.op("pe", lambda: PE.matmul(psA[:, cs], lhsT=kdT[h][:, cs], rhs=qbT[h][:, cs],
                                                         start=True, stop=True, skip_group_check=True),
                                 reads=[("h_kdT", h), ("h_qbT", h)], writes=[("h_pp", 0)])
                        S.op("dve", lambda: V.tensor_tensor(out=AmT[h][:], in0=psA[:], in1=hmask[:], op=ALU.mult),
                             reads=[("h_pp", 0), "h_hmask"], writes=[("h_AmT", h)])
                        for tq in range(4):
                            cs = slice(tq * 128, (tq + 1) * 128)
                            S.op("pe", lambda: PE.matmul(po[h][:, cs], lhsT=itok[:, tq, h * 128:(h + 1) * 128],
                                                         rhs=AmT[h][:, cs], start=(tq == 0), stop=False,
                                                         skip_group_check=True),
                                 reads=["h_itok", ("h_AmT", h)], writes=[("h_po", h)])
                    for j in range(16):
                        tq, jj = j // 4, j % 4
                        for h in range(2):
                            cur, nxt = Sb[h][j % 2], Sb[h][(j + 1) % 2]
                            ck, nk = ("h_S", h, j % 2), ("h_S", h, (j + 1) % 2)
                            js = slice(j * 32, (j + 1) * 32)
                            S.op("pe", lambda: PE.matmul(po[h][:, js], lhsT=cur[:], rhs=qbT[h][:, js], start=False,
                                                         stop=(j == 15), skip_group_check=True),
                                 reads=[ck, ("h_qbT", h)], writes=[("h_po", h)])
                            S.op("pe", lambda: PE.matmul(psS[h][:, 0:128], lhsT=identb[:], rhs=cur[:], start=True, stop=False),
                                 reads=[ck, "identb"], writes=[("h_pss", h)])
                            S.op("pe", lambda: PE.matmul(psS[h][:, 0:128], lhsT=kdm[h][:, tq, jj, :],
                                                         rhs=itok[:, tq, h * 128:(h + 1) * 128], start=False, stop=True),
                                 reads=[("h_kdm", h), "h_itok"], writes=[("h_pss", h)])
                            S.op("act", lambda: A.activation(out=nxt[:], in_=psS[h][:, 0:128], func=AF.Copy, scale=ebl[h][:, j:j + 1]),
                                 reads=[("h_pss", h), ("h_ebl", h)], writes=[nk])
                    for h in range(2):
                        S.op("dve", lambda: V.tensor_copy(out=oo[:], in_=po[h][:]), reads=[("h_po", h)], writes=["h_o"])
                        S.op("act", lambda: A.activation(out=sq[:], in_=oo[:], func=AF.Square), reads=["h_o"], writes=["h_sq"])
                        pb = psP[npp % 2]
                        key = ("h_pp", npp % 2)
                        npp += 1
                        S.op("pe", lambda: PE.matmul(pb[:], lhsT=onesb[:], rhs=sq[:], start=True, stop=True),
                             reads=["h_sq", "onesb"], writes=[key])
                        S.op("act", lambda: A.activation(out=rt2[:], in_=pb[:], func=AF.Sqrt, scale=1.0 / 128, bias=epsb[:]),
                             reads=[key, "epsb"], writes=["h_rt"])
                        S.op("dve", lambda: V.reciprocal(out=rt2[:], in_=rt2[:]), reads=["h_rt"], writes=["h_rt"])
                        S.op("dve", lambda: V.tensor_tensor(out=oo[:], in0=oo[:], in1=rt2[:], op=ALU.mult),
                             reads=["h_o", "h_rt"], writes=["h_o"])
                        S.op("dve", lambda: V.tensor_tensor(out=oo[:], in0=oo[:], in1=sgt[h][:], op=ALU.mult),
                             reads=["h_o", ("h_sgt", h)], writes=["h_o"])
                        mxt = mx[h]
                        S.op("act", lambda: A.activation(out=mxt[:], in_=oo[:], func=AF.Copy, scale=gh[:, h:h + 1]),
                             reads=["h_o", "h_gh"], writes=[("h_mx", h)])
                        S.dma("sp", f"h_mx{h}", mix_src[256 + h * 128:256 + (h + 1) * 128, sti * 512:(sti + 1) * 512], mxt[:],
                              reads=[("h_mx", h)], writes=["mix_src"])
                S.barrier()

        if active("o1"):
            with ExitStack() as st:
                hflag = sb(st, "o_hflag", [128, 1])
                S.dma("sp", "c0", hflag[:], hflag_d[:, :], writes=["o_hflag"])
                stm = ExitStack()
                mT = sb(stm, "o_mT", [128, KC, NTP], BF16)
                mxv = mixin.rearrange("(c p) t -> p c t", p=128)
                for cg in range(4):
                    S.dma("sp", "o_m", mT[:, cg * 8:(cg + 1) * 8, HB - 2:NTP], mxv[:, cg * 8:(cg + 1) * 8, :], writes=["o_mT"])
                tiles = [(0, HB - 2, 2)] + [(2 + i * 128, HB + i * 128, 128) for i in range(NT // 128)]
                with ExitStack() as st2:
                    Wo = [sb(st2, f"o_W{i}", [128, KC, 512], BF16) for i in range(1)]
                    gtb = sb(st2, "o_gtb", [128, D])
                    ostg = [sb(st2, f"o_wstg{i}", [128, 2048]) for i in range(3)]
                    xb = [sb(st2, f"o_xb{i}", [128, 512]) for i in range(2)]
                    yb = [sb(st2, f"o_yb{i}", [128, 512]) for i in range(2)]
                    psO = [pst(st2, f"o_ps{i}", [128, 512]) for i in range(2)]
                    S.dma("sp", "c0", gtb[:], modflat[:, 2 * D:3 * D].partition_broadcast(128), writes=["o_gtb"])
                    wv = wout.rearrange("(c p) n -> p c n", p=128)
                    n = 0
                    for cgr in range(D // 512):
                        W = Wo[0]
                        wk = ("o_W", 0)
                        cols = slice(cgr * 512, (cgr + 1) * 512)
                        for i in range(8):
                            wload(ostg, W[:, i * 4:(i + 1) * 4, :], wv[:, i * 4:(i + 1) * 4, cols], 4, 512, wk)
                        for (t0, s0, ntk) in tiles:
                            s2 = n % 2
                            S.dma("sp", f"o_x{s2}", xb[s2][0:ntk, :], xo[t0:t0 + ntk, cols], writes=[("o_xb", s2)])
                            for c in range(KC):
                                S.op("pe", lambda: PE.matmul(psO[s2][0:ntk, :], lhsT=mT[:, c, s0:s0 + ntk], rhs=W[:, c, :],
                                                             start=(c == 0), stop=(c == KC - 1)),
                                     reads=["o_mT", wk], writes=[("o_ps", s2)])
                            S.op("dve", lambda: V.tensor_tensor(out=yb[s2][0:ntk, :], in0=psO[s2][0:ntk, :],
                                                                in1=gtb[0:ntk, cols], op=ALU.mult),
                                 reads=[("o_ps", s2), "o_gtb"], writes=[("o_yb", s2)])
                            S.op("dve", lambda: V.tensor_tensor(out=yb[s2][0:ntk, :], in0=yb[s2][0:ntk, :],
                                                                in1=xb[s2][0:ntk, :], op=ALU.add),
                                 reads=[("o_yb", s2), ("o_xb", s2)], writes=[("o_yb", s2)])
                            S.dma("sp", f"o_y{s2}", x1_d[t0:t0 + ntk, cols], yb[s2][0:ntk, :], reads=[("o_yb", s2)],
                                  writes=["x1_d"])
                            n += 1
                    S.barrier()
                stm.close()
                h2T = sb(st, "o_h2T", [128, KC, NTP], BF16)
                if active("o2"):
                    with ExitStack() as st2:
                        xts = [sb(st2, f"o2_xt{i}", [128, D]) for i in range(2)]
                        junk = sb(st2, "o2_junk", [128, D], BF16)
                        xns = [sb(st2, f"o2_xn{i}", [128, D], BF16) for i in range(2)]
                        ss = [sb(st2, f"o2_ss{i}", [128, 1]) for i in range(2)]
                        rt = [sb(st2, f"o2_rt{i}", [128, 1]) for i in range(2)]
                        rstd = [sb(st2, f"o2_rstd{i}", [128, 1]) for i in range(2)]
                        ptp = [pst(st2, f"o2_ptp{i}", [128, 16, 128], BF16) for i in range(2)]
                        for ti, (t0, s0, ntk) in enumerate(tiles):
                            xt = xts[ti % 2]
                            S.dma("sp", f"o2_x{ti % 2}", xt[0:ntk, :], x1_d[t0:t0 + ntk, :], reads=["x1_d"],
                                  writes=[("o2_xt", ti % 2)])
                            norm_transpose([("o2_xt", ti % 2)], xt, ntk, gs2, sh2,
                                           lambda c: (h2T[:, c, s0:s0 + ntk], ["o_h2T"]),
                                           "o2", ti % 2, (junk, ss[ti % 2], rt[ti % 2], rstd[ti % 2], xns[ti % 2], ptp))
                        S.barrier()
                if active("u"):
                    with ExitStack() as st2:
                        Wu = [sb(st2, f"u_W{i}", [128, KC, 2, 256], BF16) for i in range(1)]
                        cw = sb(st2, "u_cw", [128, 2 * NFB, 3]); cb = sb(st2, "u_cb", [128, 2 * NFB])
                        ustg = [sb(st2, f"u_wstg{i}", [128, 2048]) for i in range(2)]
                        us = [sb(st2, f"u_us{i}", [128, NTH]) for i in range(2)]
                        ys = [sb(st2, f"u_ys{i}", [128, NT]) for i in range(2)]
                        sa = sb(st2, "u_sa", [128, NT])
                        gT = [sb(st2, f"u_g{i}", [128, NT], BF16) for i in range(2)]
                        pcs = _pieces(NT)
                        npc = len(pcs)
                        psU = [pst(st2, f"u_ps{i}", [128, npc, 512]) for i in range(3)]
                        psH = pst(st2, "u_ph", [128, 512])
                        S.dma("sp", "c0", cw[:], convw[:, :, :], writes=["u_cw"])
                        S.dma("sp", "c0", cb[:], convb[:, :], writes=["u_cb"])
                        wv = wup.rearrange("(c p) n -> p c n", p=128)
                        gdv = g_d.rearrange("t p f j -> p t f j")
                        n = 0
                        for fb in range(NFB):
                            if fb % 2 == 0:
                                W = Wu[0]
                                wk = ("u_W", 0)
                                for part in range(2):
                                    c0 = part * DFF + fb * 128
                                    for i in range(4):
                                        wload(ustg, W[:, i * 8:(i + 1) * 8, part, :],
                                              wv[:, i * 8:(i + 1) * 8, c0:c0 + 256], 8, 256, wk)
                            blk = fb % 2
                            for part in range(2):
                                s3 = n % 3
                                n += 1
                                ch = part * NFB + fb
                                for pi, (a0, an) in enumerate(pcs):
                                    for c in range(KC):
                                        S.op("pe", lambda: PE.matmul(psU[s3][:, pi, 0:an], lhsT=W[:, c, part, blk * 128:(blk + 1) * 128],
                                                                     rhs=h2T[:, c, HB + a0:HB + a0 + an], start=(c == 0), stop=(c == KC - 1)),
                                             reads=[wk, "o_h2T"], writes=[("u_ps", s3)])
                                for c in range(KC):
                                    S.op("pe", lambda: PE.matmul(psH[:, s3 * 2:s3 * 2 + 2], lhsT=W[:, c, part, blk * 128:(blk + 1) * 128],
                                                                 rhs=h2T[:, c, HB - 2:HB], start=(c == 0), stop=(c == KC - 1)),
                                         reads=[wk, "o_h2T"], writes=["u_ph"])
                                u = us[part]
                                uk = ("u_us", part)
                                for pi, (a0, an) in enumerate(pcs):
                                    S.op("act", lambda: A.copy(out=u[:, 2 + a0:2 + a0 + an], in_=psU[s3][:, pi, 0:an]),
                                         reads=[("u_ps", s3)], writes=[uk])
                                S.op("act", lambda: A.activation(out=u[:, 0:2], in_=psH[:, s3 * 2:s3 * 2 + 2], func=AF.Copy,
                                                                 scale=hflag[:, 0:1]),
                                     reads=["u_ph", "o_hflag"], writes=[uk])
                                y = ys[part]
                                yk = ("u_ys", part)
                                S.op("dve", lambda: V.tensor_scalar(out=y[:], in0=u[:, 2:NTH], scalar1=cw[:, ch, 2:3],
                                                                    scalar2=cb[:, ch:ch + 1], op0=ALU.mult, op1=ALU.add),
                                     reads=[uk, "u_cw", "u_cb"], writes=[yk])
                                S.op("dve", lambda: V.scalar_tensor_tensor(out=y[:], in0=u[:, 1:NTH - 1], scalar=cw[:, ch, 1:2],
                                                                           in1=y[:], op0=ALU.mult, op1=ALU.add),
                                     reads=[uk, yk, "u_cw"], writes=[yk])
                                S.op("dve", lambda: V.scalar_tensor_tensor(out=y[:], in0=u[:, 0:NTH - 2], scalar=cw[:, ch, 0:1],
                                                                           in1=y[:], op0=ALU.mult, op1=ALU.add),
                                     reads=[uk, yk, "u_cw"], writes=[yk])
                            S.op("act", lambda: A.activation(out=sa[:], in_=ys[0][:], func=AF.Silu), reads=[("u_ys", 0)], writes=["u_sa"])
                            g = gT[fb % 2]
                            S.op("dve", lambda: V.tensor_tensor(out=g[:], in0=sa[:], in1=ys[1][:], op=ALU.mult),
                                 reads=["u_sa", ("u_ys", 1)], writes=[("u_g", fb % 2)])
                            S.dma("sp", f"u_g{fb % 2}", gdv[:, :, fb, :], g[:].rearrange("p (t j) -> p t j", j=128),
                                  reads=[("u_g", fb % 2)], writes=["g_d"])
                        S.barrier()
            S.barrier()

        if active("d"):
            with ExitStack() as st:
                Wd = [sb(st, f"d_W{i}", [128, NFB, 256], BF16) for i in range(1)]
                gts = [sb(st, f"d_g{i}", [128, NFB, 128], BF16) for i in range(2)]
                gtb = sb(st, "d_gtb", [128, D])
                dstg = [sb(st, f"d_wstg{i}", [128, 2048]) for i in range(3)]
                xb = [sb(st, f"d_xb{i}", [128, 256]) for i in range(2)]
                yb = [sb(st, f"d_yb{i}", [128, 256]) for i in range(2)]
                psD = [pst(st, f"d_ps{i}", [128, 512]) for i in range(2)]
                S.dma("sp", "c0", gtb[:], modflat[:, 5 * D:6 * D].partition_broadcast(128), writes=["d_gtb"])
                wv = wdown.rearrange("(f p) n -> p f n", p=128)
                n = 0
                for cgr in range(D // 256):
                    W = Wd[0]
                    wk = ("d_W", 0)
                    cols = slice(cgr * 256, (cgr + 1) * 256)
                    for f0 in range(0, NFB, 8):
                        fn_ = min(8, NFB - f0)
                        wload(dstg, W[:, f0:f0 + fn_, :], wv[:, f0:f0 + fn_, cols], fn_, 256, wk)
                    for ti in range(NT // 128):
                        s2 = n % 2
                        n += 1
                        S.dma("sp", f"d_gl{s2}", gts[s2][:], g_d[ti, :, :, :], reads=["g_d"], writes=[("d_g", s2)])
                        S.dma("sp", f"d_x{s2}", xb[s2][:], x1_d[2 + ti * 128:2 + (ti + 1) * 128, cols], reads=["x1_d"],
                              writes=[("d_xb", s2)])
                        for f in range(NFB):
                            S.op("pe", lambda: PE.matmul(psD[s2][:, 0:256], lhsT=gts[s2][:, f, :], rhs=W[:, f, :],
                                                         start=(f == 0), stop=(f == NFB - 1)),
                                 reads=[("d_g", s2), wk], writes=[("d_ps", s2)])
                        S.op("dve", lambda: V.tensor_tensor(out=yb[s2][:], in0=psD[s2][:, 0:256], in1=gtb[:, cols], op=ALU.mult),
                             reads=[("d_ps", s2), "d_gtb"], writes=[("d_yb", s2)])
                        S.op("dve", lambda: V.tensor_tensor(out=yb[s2][:], in0=yb[s2][:], in1=xb[s2][:], op=ALU.add),
                             reads=[("d_yb", s2), ("d_xb", s2)], writes=[("d_yb", s2)])
                        S.dma("sp", f"d_y{s2}", x2_d[ti * 128:(ti + 1) * 128, cols], yb[s2][:], reads=[("d_yb", s2)],
                              writes=["x2_d"])
                S.barrier()
            with ExitStack() as st:
                xts = [sb(st, f"e_xt{i}", [128, D]) for i in range(2)]
                junk = sb(st, "e_junk", [128, D], BF16)
                gfb = sb(st, "e_gfb", [128, D])
                ss = [sb(st, f"e_ss{i}", [128, 1]) for i in range(2)]
                S.dma("sp", "c0", gfb[:], gfin[:, :].partition_broadcast(128), writes=["e_gfb"])
                for ti in range(NT // 128):
                    s2 = ti % 2
                    xt = xts[s2]
                    S.dma("sp", f"e_x{s2}", xt[:], x2_d[ti * 128:(ti + 1) * 128, :], reads=["x2_d"], writes=[("e_xt", s2)])
                    S.op("act", lambda: A.activation(out=junk[:], in_=xt[:], func=AF.Square, accum_out=ss[s2][:, 0:1]),
                         reads=[("e_xt", s2)], writes=["e_junk", ("e_ss", s2)])
                    S.op("act", lambda: A.activation(out=ss[s2][:], in_=ss[s2][:], func=AF.Sqrt, scale=1.0 / D, bias=epsb[:]),
                         reads=[("e_ss", s2), "epsb"], writes=[("e_ss", s2)])
                    S.op("dve", lambda: V.reciprocal(out=ss[s2][:], in_=ss[s2][:]), reads=[("e_ss", s2)], writes=[("e_ss", s2)])
                    S.op("dve", lambda: V.scalar_tensor_tensor(out=xt[:], in0=xt[:], scalar=ss[s2][:, 0:1], in1=gfb[:],
                                                               op0=ALU.mult, op1=ALU.mult),
                         reads=[("e_xt", s2), ("e_ss", s2), "e_gfb"], writes=[("e_xt", s2)])
                    S.dma("sp", f"e_o{s2}", out[ti * 128:(ti + 1) * 128, :], xt[:], reads=[("e_xt", s2)], writes=["out"])
        S.barrier()
    return nc


def _col(v):
    return np.ascontiguousarray(np.asarray(v, np.float32).reshape(-1, 128).T)


def prep_inputs(inp, T, upto="all"):
    NT = T // NCORES
    f32 = np.float32
    x = np.ascontiguousarray(np.asarray(inp["x"], f32)[0, :T])
    c = np.asarray(inp["c"], f32)[0]
    w_in = np.asarray(inp["w_in"], f32)[0]
    w_ada = np.asarray(inp["w_ada"], f32)[0]
    b_ada = np.asarray(inp["b_ada"], f32)[0]
    w_out = np.asarray(inp["w_out"], f32)[0]
    w_up = np.ascontiguousarray(np.asarray(inp["w_up"], f32)[0])
    w_down = np.ascontiguousarray(np.asarray(inp["w_down"], f32)[0])
    conv_w = np.asarray(inp["conv_w"], f32)[0]
    conv_b = np.asarray(inp["conv_b"], f32)[0]
    lbl = np.asarray(inp["hgrn_lb_logits"], f32)
    b_fox = np.asarray(inp["b_fox_f"], f32)[0]
    g_fox = np.asarray(inp["g_fox_out"], f32)[0]
    g_hg = np.asarray(inp["g_hgrn_out"], f32)[0]
    p = np.arange(128)
    identf = np.eye(128, dtype=f32)
    identb = identf.astype(ml_dtypes.bfloat16)
    trif = (p[:, None] <= p[None, :]).astype(f32)
    fmask = np.where(p[:, None] <= p[None, :], 0.0, NEG).astype(f32)
    hm = ((p[:, None] // 32 == p[None, :] // 32) & (p[:, None] <= p[None, :])).astype(f32)
    hmask = np.tile(hm, (1, 4)).astype(ml_dtypes.bfloat16)
    cmask = np.tile((np.arange(512) % 32 != 0).astype(f32)[None, :], (128, 1))
    rowmask = (p[:, None] // 32 == np.arange(4)[None, :]).astype(f32)
    perm = np.concatenate([np.concatenate([np.arange(256 * r, 256 * r + 256),
                                           np.arange(2048 + 256 * r, 2048 + 256 * r + 256)]) for r in range(NCORES)])
    wout_p = np.ascontiguousarray(w_out[perm])
    convw = np.ascontiguousarray(conv_w.T.reshape(2 * NFB, 128, 3).transpose(1, 0, 2))
    convb = np.ascontiguousarray(conv_b.reshape(2 * NFB, 128).T)
    lim = ["p0", "n", "f", "h", "x", "o1", "o2", "u", "d", "all"].index(upto)
    stub = np.zeros((128, 128), f32)
    if lim < 5:
        wout_p = stub
    if lim < 7:
        w_up = stub
    if lim < 8:
        w_down = stub
    shared = dict(x=x, ccol=_col(c), gmix=_col(inp["g_mix_norm"][0]), gffn=_col(inp["g_ffn_norm"][0]),
                  gfin=np.asarray(inp["g_final"], f32).reshape(1, D), wout=wout_p, wup=w_up, wdown=w_down,
                  convw=convw, convb=convb, identb=identb, identf=identf, trif=trif, fmask=fmask, hmask=hmask,
                  cmask=cmask, rowmask=rowmask)
    maps = []
    for i in range(NCORES):
        hd = [2 * i, 2 * i + 1]
        fc = []
        for base in (0, 2048, 4096):
            for h in hd:
                fc.append(np.arange(base + 128 * h, base + 128 * h + 128))
        fc.append(np.array([6144 + hd[0], 6144 + hd[1]]))
        fc = np.concatenate(fc)
        hc = []
        for base in (6160, 6160 + 2048, 6160 + 6144, 6160 + 4096):
            for h in hd:
                hc.append(np.arange(base + 128 * h, base + 128 * h + 128))
        hc = np.concatenate(hc)
        xo = np.zeros((NT + 2, D), f32)
        xo[2:] = x[i * NT:(i + 1) * NT]
        if i > 0:
            xo[:2] = x[i * NT - 2:i * NT]
        sel = np.zeros((128, 8), f32)
        sel[:, i] = 1.0
        m = dict(shared)
        m.update(
            xo=xo,
            wada=np.ascontiguousarray(w_ada[:, i * 3072:(i + 1) * 3072]),
            bada=np.ascontiguousarray(b_ada[i * 3072:(i + 1) * 3072].reshape(1, 3072)),
            wfox=np.ascontiguousarray(w_in[:, fc]), whg=np.ascontiguousarray(w_in[:, hc]),
            nbfox=np.ascontiguousarray(np.tile(-b_fox[hd][None, :], (128, 1))),
            lbl=np.ascontiguousarray(np.stack([lbl[0, 128 * hd[0]:128 * hd[0] + 128], lbl[0, 128 * hd[1]:128 * hd[1] + 128],
                                               lbl[1, 128 * hd[0]:128 * hd[0] + 128], lbl[1, 128 * hd[1]:128 * hd[1] + 128]], axis=1)),
            gfox=np.ascontiguousarray(np.stack([g_fox[128 * h:128 * h + 128] for h in hd], axis=1)),
            ghg=np.ascontiguousarray(np.stack([g_hg[128 * h:128 * h + 128] for h in hd], axis=1)),
            sel=sel, hflag=np.full((128, 1), 0.0 if i == 0 else 1.0, f32),
        )
        maps.append(m)
    return maps


def _run(nc, maps, phase):
    need = NEED[phase]
    ms = [{k: v for k, v in m.items() if k in need} for m in maps]
    return run_bass_kernel_spmd(nc, ms, core_ids=list(range(NCORES))).results


def kernel(**inputs):
    T = 8192
    NT = T // NCORES
    maps = prep_inputs(inputs, T)
    ra = _run(build_program(T, phase="A"), maps, "A")
    modall = np.concatenate([np.asarray(r["modsrc"], np.float32).reshape(1, 3072) for r in ra], axis=0)
    for m in maps:
        m["modall_in"] = modall
    rb = _run(build_program(T, phase="B"), maps, "B")
    mix_all = np.concatenate([np.asarray(r["mix_src"]) for r in rb], axis=0)
    for i, m in enumerate(maps):
        mi = np.zeros((D, NT + 2), mix_all.dtype)
        mi[:, 2:] = mix_all[:, i * NT:(i + 1) * NT]
        if i > 0:
            mi[:, :2] = mix_all[:, i * NT - 2:i * NT]
        m["mixin"] = mi
    rc = _run(build_program(T, phase="C"), maps, "C")
    outs = [np.asarray(r["out"]) for r in rc]
    return np.concatenate(outs, axis=0).reshape(1, T, D).astype(np.float32)
```
